# Optimizing a Trainium2 kernel written in Bass

```python
import jax, jax.numpy as jnp
from jax import lax
import numpy as np

D_MODEL = 4096
BATCH = 2
SEQ = 8192
DEPTH = 4

N_MIXERS = 2
N_LAYERS_A = (DEPTH + 1) // 2
N_LAYERS_B = DEPTH // 2
D_FF = 7 * D_MODEL // 4
MLA_HEADS = 32
QK_NOPE = 128
QK_ROPE = 64
V_HEAD = 128
Q_LORA = D_MODEL // 4
KV_LORA = D_MODEL // 8
ROPE_THETA = 10000.0
Q_BLOCK = 128
LRU_WIDTH = D_MODEL
LRU_BLOCKS = 16
LRU_BLOCK = LRU_WIDTH // LRU_BLOCKS
CONV_WIDTH = 4
LRU_C = 8.0
NORM_EPS = 1e-6

kernel_name = "hybrid_mla_rglru_macaron_trunk"


def rms_norm(x, g):
    x32 = x.astype(jnp.float32)
    y = x32 * lax.rsqrt(jnp.mean(x32 * x32, axis=-1, keepdims=True) + NORM_EPS)
    return (y * g.astype(jnp.float32)).astype(x.dtype)


def swiglu_ffn(x, w_in, w_out):
    gate, up = jnp.split(x @ w_in, 2, axis=-1)
    return (jax.nn.silu(gate) * up) @ w_out


def rope(x, positions):
    half = x.shape[-1] // 2
    inv_freq = ROPE_THETA ** (-jnp.arange(half, dtype=jnp.float32) / half)
    ang = positions.astype(jnp.float32)[..., None] * inv_freq
    ang = ang.reshape(ang.shape[:2] + (1,) * (x.ndim - 3) + (half,))
    cos, sin = jnp.cos(ang), jnp.sin(ang)
    x1 = x[..., :half].astype(jnp.float32)
    x2 = x[..., half:].astype(jnp.float32)
    return jnp.concatenate([x1 * cos - x2 * sin, x2 * cos + x1 * sin], axis=-1).astype(x.dtype)


def mla_mixer(u, positions, w_in, q_norm, kv_norm, w_uq, w_ukv, w_o):
    B, S, _ = u.shape
    c = u @ w_in
    c_q, c_kv, k_rope = jnp.split(c, [Q_LORA, Q_LORA + KV_LORA], axis=-1)
    q = (rms_norm(c_q, q_norm) @ w_uq).reshape(B, S, MLA_HEADS, QK_NOPE + QK_ROPE)
    q_nope, q_rope = jnp.split(q, [QK_NOPE], axis=-1)
    q_rope = rope(q_rope, positions)
    kv = (rms_norm(c_kv, kv_norm) @ w_ukv).reshape(B, S, MLA_HEADS, QK_NOPE + V_HEAD)
    k_nope, v = jnp.split(kv, [QK_NOPE], axis=-1)
    k_rope = rope(k_rope, positions)
    scale = (QK_NOPE + QK_ROPE) ** -0.5
    n_blk = S // Q_BLOCK
    qn_blocks = q_nope.reshape(B, n_blk, Q_BLOCK, MLA_HEADS, QK_NOPE).transpose(1, 0, 2, 3, 4)
    qr_blocks = q_rope.reshape(B, n_blk, Q_BLOCK, MLA_HEADS, QK_ROPE).transpose(1, 0, 2, 3, 4)
    starts = jnp.arange(n_blk, dtype=jnp.int32) * Q_BLOCK
    key_pos = jnp.arange(S, dtype=jnp.int32)
    neg = jnp.finfo(jnp.float32).min

    def attend(args):
        qn, qr, start = args
        s = jnp.einsum('bqhd,bkhd->bhqk', qn, k_nope, preferred_element_type=jnp.float32)
        s = s + jnp.einsum('bqhr,bkr->bhqk', qr, k_rope, preferred_element_type=jnp.float32)
        q_pos = start + jnp.arange(Q_BLOCK, dtype=jnp.int32)
        causal = key_pos[None, :] <= q_pos[:, None]
        s = jnp.where(causal, s * scale, neg)
        p = jax.nn.softmax(s, axis=-1).astype(v.dtype)
        return jnp.einsum('bhqk,bkhd->bqhd', p, v)

    o = lax.map(attend, (qn_blocks, qr_blocks, starts))
    o = o.transpose(1, 0, 2, 3, 4).reshape(B, S, MLA_HEADS * V_HEAD)
    return o @ w_o


def rglru_mixer(u, w_in, conv_w, conv_b, gate_a_w, gate_a_b, gate_x_w, gate_x_b, a_param, w_out):
    B, S, _ = u.shape
    y_branch, x_branch = jnp.split(u @ w_in, 2, axis=-1)
    y_branch = jax.nn.gelu(y_branch, approximate=True)
    x_branch = lax.conv_general_dilated(
        x_branch, conv_w[:, None, :], window_strides=(1,),
        padding=[(CONV_WIDTH - 1, 0)], dimension_numbers=('NWC', 'WIO', 'NWC'),
        feature_group_count=LRU_WIDTH) + conv_b
    xb = x_branch.reshape(B, S, LRU_BLOCKS, LRU_BLOCK)
    gate_r = jax.nn.sigmoid(jnp.einsum('bsni,nij->bsnj', xb, gate_a_w) + gate_a_b)
    gate_i = jax.nn.sigmoid(jnp.einsum('bsni,nij->bsnj', xb, gate_x_w) + gate_x_b)
    gate_r = gate_r.reshape(B, S, LRU_WIDTH).astype(jnp.float32)
    gate_i = gate_i.reshape(B, S, LRU_WIDTH).astype(jnp.float32)
    log_a = -LRU_C * gate_r * jax.nn.softplus(-a_param.astype(jnp.float32))
    a = jnp.exp(log_a)
    mult = jnp.sqrt(-jnp.expm1(2.0 * log_a))
    b = mult * (gate_i * x_branch.astype(jnp.float32))

    def combine(left, right):
        a_l, b_l = left
        a_r, b_r = right
        return a_l * a_r, a_r * b_l + b_r

    _, h = lax.associative_scan(combine, (a, b), axis=1)
    return (h.astype(u.dtype) * y_branch) @ w_out


def _normal(key, shape, fan_in):
    return jax.random.normal(key, shape, jnp.float32) * (fan_in ** -0.5)


def _gain(key, shape):
    return 1.0 + 0.02 * jax.random.normal(key, shape, jnp.float32)


def setup_inputs(seed: int = 0) -> dict:
    key = jax.random.key(seed)
    ks = jax.random.split(key, 30)
    x = jax.random.normal(ks[0], (BATCH, SEQ, D_MODEL), jnp.float32)
    offset = jax.random.randint(ks[1], (BATCH, 1), 0, 1024, dtype=jnp.int32)
    positions = offset + jnp.arange(SEQ, dtype=jnp.int32)[None, :]
    u = jax.random.uniform(ks[2], (N_LAYERS_B, LRU_WIDTH), jnp.float32, 0.9, 0.999)
    s = u ** (1.0 / LRU_C)
    return {
        "x": x,
        "positions": positions,
        "norm_ffn1": _gain(ks[3], (DEPTH, D_MODEL)),
        "ffn1_in": _normal(ks[4], (DEPTH, D_MODEL, 2 * D_FF), D_MODEL),
        "ffn1_out": _normal(ks[5], (DEPTH, D_FF, D_MODEL), D_FF),
        "norm_mix": _gain(ks[6], (DEPTH, D_MODEL)),
        "norm_ffn2": _gain(ks[7], (DEPTH, D_MODEL)),
        "ffn2_in": _normal(ks[8], (DEPTH, D_MODEL, 2 * D_FF), D_MODEL),
        "ffn2_out": _normal(ks[9], (DEPTH, D_FF, D_MODEL), D_FF),
        "mla_in": _normal(ks[10], (N_LAYERS_A, D_MODEL, Q_LORA + KV_LORA + QK_ROPE), D_MODEL),
        "mla_q_norm": _gain(ks[11], (N_LAYERS_A, Q_LORA)),
        "mla_kv_norm": _gain(ks[12], (N_LAYERS_A, KV_LORA)),
        "mla_w_uq": _normal(ks[13], (N_LAYERS_A, Q_LORA, MLA_HEADS * (QK_NOPE + QK_ROPE)), Q_LORA),
        "mla_w_ukv": _normal(ks[14], (N_LAYERS_A, KV_LORA, MLA_HEADS * (QK_NOPE + V_HEAD)), KV_LORA),
        "mla_w_o": _normal(ks[15], (N_LAYERS_A, MLA_HEADS * V_HEAD, D_MODEL), MLA_HEADS * V_HEAD),
        "rg_in": _normal(ks[16], (N_LAYERS_B, D_MODEL, 2 * LRU_WIDTH), D_MODEL),
        "rg_conv_w": _normal(ks[17], (N_LAYERS_B, CONV_WIDTH, LRU_WIDTH), CONV_WIDTH),
        "rg_conv_b": 0.01 * jax.random.normal(ks[18], (N_LAYERS_B, LRU_WIDTH), jnp.float32),
        "rg_gate_a_w": _normal(ks[19], (N_LAYERS_B, LRU_BLOCKS, LRU_BLOCK, LRU_BLOCK), LRU_BLOCK),
        "rg_gate_a_b": 0.01 * jax.random.normal(ks[20], (N_LAYERS_B, LRU_BLOCKS, LRU_BLOCK), jnp.float32),
        "rg_gate_x_w": _normal(ks[21], (N_LAYERS_B, LRU_BLOCKS, LRU_BLOCK, LRU_BLOCK), LRU_BLOCK),
        "rg_gate_x_b": 0.01 * jax.random.normal(ks[22], (N_LAYERS_B, LRU_BLOCKS, LRU_BLOCK), jnp.float32),
        "rg_a_param": jnp.log(s) - jnp.log1p(-s),
        "rg_out": _normal(ks[23], (N_LAYERS_B, LRU_WIDTH, D_MODEL), LRU_WIDTH),
        "norm_final": _gain(ks[24], (D_MODEL,)),
    }


def reference(x, positions, norm_ffn1, ffn1_in, ffn1_out, norm_mix, norm_ffn2, ffn2_in, ffn2_out,
              mla_in, mla_q_norm, mla_kv_norm, mla_w_uq, mla_w_ukv, mla_w_o,
              rg_in, rg_conv_w, rg_conv_b, rg_gate_a_w, rg_gate_a_b, rg_gate_x_w, rg_gate_x_b,
              rg_a_param, rg_out, norm_final):
    h = x
    for i in range(DEPTH):
        h = h + 0.5 * swiglu_ffn(rms_norm(h, norm_ffn1[i]), ffn1_in[i], ffn1_out[i])
        u = rms_norm(h, norm_mix[i])
        j = i // N_MIXERS
        if i % N_MIXERS == 0:
            m = mla_mixer(u, positions, mla_in[j], mla_q_norm[j], mla_kv_norm[j],
                          mla_w_uq[j], mla_w_ukv[j], mla_w_o[j])
        else:
            m = rglru_mixer(u, rg_in[j], rg_conv_w[j], rg_conv_b[j], rg_gate_a_w[j], rg_gate_a_b[j],
                            rg_gate_x_w[j], rg_gate_x_b[j], rg_a_param[j], rg_out[j])
        h = h + m
        h = h + 0.5 * swiglu_ffn(rms_norm(h, norm_ffn2[i]), ffn2_in[i], ffn2_out[i])
    return rms_norm(h, norm_final)
```

```python
import math
from collections import defaultdict
from contextlib import ExitStack

import numpy as np
import ml_dtypes
import concourse.bass as bass
import concourse.mybir as mybir
from concourse.bass_utils import run_bass_kernel_spmd

F32 = mybir.dt.float32
BF16 = mybir.dt.bfloat16
I32 = mybir.dt.int32
AF = mybir.ActivationFunctionType
ALU = mybir.AluOpType
TS = 512


class Cfg:
    def __init__(self, D=4096, F=7168, H=32, QL=1024, KVL=512, NT=2048, G=4, NCORES=8, DEPTH=4, NB=16):
        self.D, self.F, self.H, self.QL, self.KVL, self.NT, self.G, self.NCORES, self.DEPTH, self.NB = \
            D, F, H, QL, KVL, NT, G, NCORES, DEPTH, NB
        self.KC, self.FC, self.QLC, self.KVC = D // 128, F // 128, QL // 128, KVL // 128
        self.LW = D
        self.LC = D // 128
        self.S = G * NT
        self.NTT = NT // TS
        self.NA = (DEPTH + 1) // 2
        self.NR = DEPTH // 2
        self.LATR = KVL + 64
        assert H * 128 == D and self.LW // NB == 256
        o = 0
        self.sm = {}

        def add(name, n):
            nonlocal o
            self.sm[name] = o
            o += n
        for i in range(DEPTH):
            add(("g1", i), self.KC); add(("gm", i), self.KC); add(("g2", i), self.KC)
        for j in range(self.NA):
            add(("qn", j), self.QLC); add(("kvn", j), self.KVC)
        for j in range(self.NR):
            add(("cw", j), 4 * self.LC); add(("cb", j), self.LC); add(("gab", j), self.LC)
            add(("gxb", j), self.LC); add(("ap", j), self.LC)
        add("gf", self.KC)
        self.NSM = o
        self.c_invf, self.c_sgn, self.c_eps, self.c_one, self.c_halfpi, self.c_zero = 0, 1, 2, 3, 4, 5
        self.c_kbias = 8
        self.c_selprev = 8 + G
        self.c_selbef = 8 + 2 * G
        self.NCST = 8 + 3 * G
        self.FSPL = 4 if self.FC % 4 == 0 else (2 if self.FC % 2 == 0 else 1)
        self.NS = 2 if self.NTT % 2 == 0 else 1


class KB:
    def __init__(self, nc, es, ndma=8):
        self.nc = nc
        self.eng = {"pe": nc.tensor, "act": nc.scalar, "dve": nc.vector, "pool": nc.gpsimd, "sp": nc.sync}
        self.semobj = {}
        for e in ["pe", "act", "dve", "pool"]:
            self.semobj[e] = es.enter_context(nc.semaphore("s_" + e))
        self.cnt = defaultdict(int)
        self.ndma = ndma
        self.dq = {}
        for q, issuer in [("w", "pool"), ("io", "sp")]:
            self.dq[q] = {"issuer": issuer, "n": 0}
            for s in range(ndma):
                self.semobj[(q, s)] = es.enter_context(nc.semaphore(f"d_{q}{s}"))
        self.es = es
        self.ncc = 0
        self.epoch = None
        self.waited = defaultdict(dict)
        self.lw = {}
        self.rd = defaultdict(dict)
        self.ninst = 0

    def _deps(self, reads, writes):
        raw, oth = {}, {}
        for r in reads:
            t = self.lw.get(r, self.epoch)
            if t is not None:
                k, v = t
                raw[k] = max(raw.get(k, 0), v)
        for w in writes:
            t = self.lw.get(w, self.epoch)
            if t is not None:
                k, v = t
                oth[k] = max(oth.get(k, 0), v)
            for k, v in self.rd.get(w, {}).items():
                oth[k] = max(oth.get(k, 0), v)
        return raw, oth

    def _wait(self, waiter, raw, oth):
        allk = set(raw) | set(oth)
        for k in allk:
            if k == waiter:
                if waiter == "pe" or k not in raw:
                    continue
                v = raw[k]
            else:
                v = max(raw.get(k, 0), oth.get(k, 0))
            if self.waited[waiter].get(k, 0) >= v:
                continue
            self.eng[waiter].wait_ge(self.semobj[k], v)
            self.waited[waiter][k] = v

    def _record(self, tick, reads, writes):
        k, v = tick
        for r in reads:
            self.rd[r][k] = v
        for w in writes:
            self.lw[w] = tick
            self.rd[w] = {}

    def op(self, e, fn, reads=(), writes=()):
        raw, oth = self._deps(reads, writes)
        self._wait(e, raw, oth)
        inst = fn(self.eng[e])
        self.cnt[e] += 1
        inst.then_inc(self.semobj[e], 1)
        self._record((e, self.cnt[e]), reads, writes)
        self.ninst += 1

    def mm_group(self, mms, reads, writes, start=True, stop=True):
        raw, oth = self._deps(reads, writes)
        self._wait("pe", raw, oth)
        n = len(mms)
        inst = None
        for i, (o, l, r) in enumerate(mms):
            inst = self.nc.tensor.matmul(o, lhsT=l, rhs=r, start=(start and i == 0), stop=(stop and i == n - 1))
        self.cnt["pe"] += 1
        inst.then_inc(self.semobj["pe"], 1)
        self._record(("pe", self.cnt["pe"]), reads, writes)
        self.ninst += n

    def dma(self, q, out, in_, reads=(), writes=()):
        Q = self.dq[q]
        issuer = Q["issuer"]
        i = Q["n"]
        Q["n"] += 1
        key = (q, i % self.ndma)
        need = 16 * (i // self.ndma)
        raw, oth = self._deps(reads, writes)
        if need > 0:
            oth[key] = max(oth.get(key, 0), need)
        self._wait(issuer, raw, oth)
        inst = self.eng[issuer].dma_start(out=out, in_=in_)
        inst.then_inc(self.semobj[key], 16)
        self._record((key, need + 16), reads, writes)
        self.ninst += 1

    def collective(self, ins, outs, groups, reads, writes):
        raw, oth = self._deps(reads, writes)
        self._wait("pool", raw, oth)
        key = ("cc", self.ncc)
        self.ncc += 1
        self.semobj[key] = self.es.enter_context(self.nc.semaphore(f"s_cc{self.ncc}"))
        self.nc.gpsimd.collective_compute("AllGather", ALU.bypass, replica_groups=groups,
                                          ins=[ins], outs=[outs]).then_inc(self.semobj[key])
        self._record((key, 1), reads, writes)

    def barrier(self, dummy_ap, pred=None):
        pred = pred or (lambda r: isinstance(r, tuple) and r[0] == "A")
        old = [r for r in set(self.lw) | set(self.rd) if pred(r)]
        self.op("dve", lambda e: e.memset(dummy_ap, 0.0), reads=(), writes=old + ["__dummy"])
        for r in old:
            self.lw.pop(r, None)
            self.rd.pop(r, None)
        self.epoch = ("dve", self.cnt["dve"])

    def finish(self, res):
        raw, oth = self._deps(res, ())
        self._wait("sp", raw, oth)


class DT:
    def __init__(self, ap, name):
        self.ap, self.name = ap, name


def build(cfg):
    c = cfg
    nc = bass.Bass("TRN2", target_bir_lowering=False)
    D, F, H, QL, KVL, NT, G, S = c.D, c.F, c.H, c.QL, c.KVL, c.NT, c.G, c.S
    KC, FC, QLC, KVC, LC, NTT = c.KC, c.FC, c.QLC, c.KVC, c.LC, c.NTT

    FSPL, FQ = c.FSPL, c.FC // c.FSPL

    def din(name, shape, dt=F32):
        return nc.dram_tensor(name, list(shape), dt, kind="ExternalInput")

    def dscr(name, shape, dt=F32):
        return nc.dram_tensor(name, list(shape), dt)

    xT = DT(din("xT", [D, NT]).ap(), "xT")
    pos = din("pos", [1, NT], I32)
    smalls = din("smalls", [128, c.NSM]).ap()
    cst = din("cst", [128, c.NCST]).ap()
    masks = din("masks", [128, G * 2 * TS], BF16).ap()
    W = {}
    for i in range(c.DEPTH):
        W["f1i", i] = din(f"f1i{i}", [2 * FC, 128, KC * 128]).ap()
        W["f1o", i] = din(f"f1o{i}", [KC * FSPL, 128, FQ * 128]).ap()
        W["f2i", i] = din(f"f2i{i}", [2 * FC, 128, KC * 128]).ap()
        W["f2o", i] = din(f"f2o{i}", [KC * FSPL, 128, FQ * 128]).ap()
    for j in range(c.NA):
        W["min", j] = din(f"min{j}", [D, QL + KVL + 128]).ap()
        W["muq", j] = din(f"muq{j}", [QL, H * 256]).ap()
        W["mukv", j] = din(f"mukv{j}", [KVL, H * 256]).ap()
        W["mo", j] = din(f"mo{j}", [KC, 128, H * 128]).ap()
    for j in range(c.NR):
        W["rin", j] = din(f"rin{j}", [2 * LC, 128, KC * 128]).ap()
        W["rga", j] = din(f"rga{j}", [c.NB * 256, 256]).ap()
        W["rgx", j] = din(f"rgx{j}", [c.NB * 256, 256]).ap()
        W["rout", j] = din(f"rout{j}", [KC, 128, LC * 128]).ap()
    outT = nc.dram_tensor("outT", [D, NT], F32, kind="ExternalOutput").ap()
    hT = DT(dscr("hT", [D, NT]).ap(), "hT")
    cqn_d = DT(dscr("cqn_d", [QL, NT], BF16).ap(), "cqn")
    lat_t = [[dscr(f"lat_d{j}_{t}", [c.LATR, TS], BF16) for t in range(NTT)] for j in range(c.NA)]
    latg_t = [[dscr(f"latg_d{j}_{t}", [G * c.LATR, TS], BF16) for t in range(NTT)] for j in range(c.NA)]
    o_d = DT(dscr("o_d", [H * 128, NT], BF16).ap(), "o_d")
    y_d = DT(dscr("y_d", [c.LW, NT]).ap(), "y_d")
    hl_d = DT(dscr("hl_d", [c.LW, NT]).ap(), "hl_d")
    P_d = DT(dscr("P_d", [c.LW, NT]).ap(), "P_d")
    xp_d = DT(dscr("xp_d", [c.LW, NT + 4]).ap(), "xp_d")
    halo_t_d = [dscr(f"halo_d{j}", [128, LC * 4]) for j in range(c.NR)]
    halog_t = [dscr(f"halog_d{j}", [G * 128, LC * 4]) for j in range(c.NR)]
    car_t = [dscr(f"car_d{j}", [128, 2 * LC]) for j in range(c.NR)]
    carg_t = [dscr(f"carg_d{j}", [G * 128, 2 * LC]) for j in range(c.NR)]
    groups = [list(range(b * G, (b + 1) * G)) for b in range(c.NCORES // G)]
    WSLOT = max(KC * 128, FQ * 128, QLC * 256, KVC * 256, 2 * 256)

    with ExitStack() as es:
        E = es.enter_context
        kb = KB(nc, es)

        def sb(name, shape, dt=F32):
            return E(nc.sbuf_tensor(name, list(shape), dt))

        PS = [E(nc.psum_tensor(f"ps{i}", [128, TS], F32)) for i in range(8)]
        sm = sb("sm", [128, c.NSM])
        ct = sb("ct", [128, c.NCST])
        ones = sb("ones", [128, 128], BF16)
        zer = sb("zer", [128, TS])
        dummy = sb("dummyt", [128, 2])
        cs1 = sb("cs1", [64, NT])
        cs2 = sb("cs2", [64, NT])
        CS1, CS2 = ("cs", 1), ("cs", 0)
        rr = defaultdict(int)

        def rot(name, n):
            i = rr[name] % n
            rr[name] += 1
            return i

        held = set()

        def bank(hold=False):
            while True:
                b = rot("bank", 8)
                if b not in held:
                    break
            if hold:
                held.add(b)
            return b

        def C(col, n=1):
            return ct[:, col:col + n]

        kb.dma("io", sm[:], smalls, writes=["sm"])
        kb.dma("io", ct[:], cst, writes=["ct"])
        kb.op("dve", lambda e: e.memset(ones[:], 1.0), writes=["ones"])
        kb.op("dve", lambda e: e.memset(zer[:], 0.0), writes=["zer"])

        def rope_tables():
            with nc.sbuf_tensor("posi", [64, NT], I32) as pi_t, nc.sbuf_tensor("ang", [64, NT], F32) as ang, \
                    nc.sbuf_tensor("rt_t", [64, NT], F32) as t, nc.sbuf_tensor("rt_ni", [64, NT], I32) as n_i, \
                    nc.sbuf_tensor("rt_nf", [64, NT], F32) as n_f, nc.sbuf_tensor("rt_d", [64, NT], F32) as d:
                R = lambda x: ("R", x)
                kb.dma("io", pi_t[:], bass.AP(pos, 0, [[0, 64], [1, NT]]), writes=[R("posi")])
                kb.op("dve", lambda e: e.tensor_copy(out=ang[:], in_=pi_t[:]), reads=[R("posi")], writes=[R("ang")])
                kb.op("dve", lambda e: e.tensor_scalar(out=ang[:], in0=ang[:], scalar1=ct[0:64, c.c_invf:c.c_invf + 1],
                                                       scalar2=None, op0=ALU.mult), reads=[R("ang"), "ct"], writes=[R("ang")])
                C1, C2 = 6.28125, 2 * math.pi - 6.28125
                for which, dst in ((0, cs2), (1, cs1)):
                    off = 0.25 * which
                    kb.op("dve", lambda e: e.tensor_scalar(out=t[:], in0=ang[:], scalar1=1.0 / (2 * math.pi), scalar2=off,
                                                           op0=ALU.mult, op1=ALU.add), reads=[R("ang")], writes=[R("t")])
                    kb.op("dve", lambda e: e.tensor_copy(out=n_i[:], in_=t[:]), reads=[R("t")], writes=[R("ni")])
                    kb.op("dve", lambda e: e.tensor_copy(out=n_f[:], in_=n_i[:]), reads=[R("ni")], writes=[R("nf")])
                    kb.op("dve", lambda e: e.tensor_tensor(out=d[:], in0=t[:], in1=n_f[:], op=ALU.subtract),
                          reads=[R("t"), R("nf")], writes=[R("d")])
                    kb.op("dve", lambda e: e.scalar_tensor_tensor(out=n_f[:], in0=d[:], scalar=0.5, in1=n_f[:],
                                                                  op0=ALU.is_gt, op1=ALU.add), reads=[R("d"), R("nf")], writes=[R("nf")])
                    kb.op("dve", lambda e: e.tensor_scalar(out=d[:], in0=d[:], scalar1=-0.5, scalar2=None, op0=ALU.is_lt),
                          reads=[R("d")], writes=[R("d")])
                    kb.op("dve", lambda e: e.tensor_tensor(out=n_f[:], in0=n_f[:], in1=d[:], op=ALU.subtract),
                          reads=[R("d"), R("nf")], writes=[R("nf")])
                    kb.op("dve", lambda e: e.scalar_tensor_tensor(out=t[:], in0=n_f[:], scalar=-C1, in1=ang[:],
                                                                  op0=ALU.mult, op1=ALU.add), reads=[R("nf"), R("ang")], writes=[R("t")])
                    kb.op("dve", lambda e: e.scalar_tensor_tensor(out=t[:], in0=n_f[:], scalar=-C2, in1=t[:],
                                                                  op0=ALU.mult, op1=ALU.add), reads=[R("nf"), R("t")], writes=[R("t")])
                    kb.op("dve", lambda e: e.tensor_scalar(out=t[:], in0=t[:], scalar1=(math.pi / 2) * which, scalar2=math.pi,
                                                           op0=ALU.add, op1=ALU.min), reads=[R("t")], writes=[R("t")])
                    kb.op("dve", lambda e: e.tensor_scalar(out=t[:], in0=t[:], scalar1=-math.pi, scalar2=None, op0=ALU.max),
                          reads=[R("t")], writes=[R("t")])
                    kb.op("act", lambda e: e.activation(out=dst[:], in_=t[:], func=AF.Sin), reads=[R("t")], writes=[("cs", which)])
                kb.op("dve", lambda e: e.tensor_scalar(out=cs2[:], in0=cs2[:], scalar1=ct[0:64, c.c_sgn:c.c_sgn + 1],
                                                       scalar2=None, op0=ALU.mult), reads=[CS2, "ct"], writes=[CS2])
                kb.barrier(dummy[:, 0:1], pred=lambda r: isinstance(r, tuple) and r[0] == "R")

        if c.NA > 0:
            rope_tables()

        NW = 4
        wsl = [sb(f"w{i}", [128, WSLOT], BF16) for i in range(NW)]
        NHB = 4
        hb = [sb(f"hb{i}", [128, TS]) for i in range(NHB)]
        NTMP = 6
        tmp = [sb(f"tmp{i}", [128, TS]) for i in range(NTMP)]
        sqt = [sb(f"sq{i}", [128, TS], BF16) for i in range(2)]
        rstd = sb("rstd", [128, TS])
        rstd2 = sb("rstd2", [128, TS])
        pt = [sb(f"pt{i}", [128, TS], BF16) for i in range(3)]
        ot = [sb(f"ot{i}", [128, TS], BF16) for i in range(2)]
        if c.NR > 0:
            sp8 = sb("sp8", [128, LC])
            halo_t = sb("halo_t", [128, LC, 4])
            st_h = sb("st_h", [128, 2 * LC])
            hin = sb("hin", [128, LC])
            dlt = sb("dlt", [128, LC])
            xpt = [sb(f"xpt{i}", [128, TS + 4]) for i in range(2)]
            gt = [sb(f"gt{i}", [128, TS]) for i in range(3)]
            xbf = [sb(f"xbf{i}", [128, TS]) for i in range(2)]
            hg = sb("hg", [128, G, LC * 4])
            hs = sb("hs", [128, LC * 4])
            cg = sb("cg", [128, G, 2 * LC])
        A_FFN = (KC + FQ) * TS * c.NS
        A_M1 = KC * TS + max(QLC, KVC) * TS * 2 + max(QLC, KVC) * TS + TS
        A_M2 = 2 * KVC * TS + 2 * QLC * TS + S + S + S + 2 * NT + G * 2 * TS
        A_RG = KC * TS + 2 * TS
        ARENA = max(A_FFN, A_M1, A_M2, A_RG)
        arena = sb("arena", [128, ARENA], BF16)

        def arena_switch():
            kb.barrier(dummy[:, 0:1])

        class Carver:
            def __init__(self):
                self.o = 0

            def __call__(self, n):
                a = arena[:, self.o:self.o + n]
                self.o += n
                return a

        def load_w(view, kcn, ncols, slot):
            dst = wsl[slot][:, 0:kcn * ncols].rearrange("p (k n) -> p k n", n=ncols)
            kb.dma("w", dst, view.rearrange("(k p) n -> p k n", p=128), writes=[("w", slot)])
            return dst

        def load_wb(wblk, blk, kcn, ncols, slot):
            n = kcn * ncols
            bsz = n
            while bsz > 2048 or n % bsz:
                bsz -= 128
            kb.dma("w", wsl[slot][:, 0:n].rearrange("p (a b) -> p a b", b=bsz),
                   wblk[blk].rearrange("p (a b) -> p a b", b=bsz), writes=[("w", slot)])
            return wsl[slot][:, 0:n].rearrange("p (k n) -> p k n", n=ncols)

        def rstd_from_bank(b, nfeat, dst, name):
            kb.op("act", lambda e: e.activation(out=dst[:], in_=PS[b][:], func=AF.Sqrt, bias=C(c.c_eps), scale=1.0 / nfeat),
                  reads=[("ps", b), "ct"], writes=[name])
            kb.op("dve", lambda e: e.reciprocal(out=dst[:], in_=dst[:]), reads=[name], writes=[name])

        def sumsq_accum(b, src_ap, src_res, first, last):
            i = rot("sq", 2)
            kb.op("act", lambda e: e.activation(out=sqt[i][:], in_=src_ap, func=AF.Square), reads=[src_res], writes=[("sq", i)])
            kb.mm_group([(PS[b][:], ones[:], sqt[i][:])], reads=[("sq", i), "ones"], writes=[("ps", b)], start=first, stop=last)

        def norm_from_dram(src, tt, gname, emit):
            b = bank(hold=True)
            tok = slice(tt * TS, (tt + 1) * TS)
            for kc in range(KC):
                i = rot("hb", NHB)
                kb.dma("io", hb[i][:], src.ap[kc * 128:(kc + 1) * 128, tok], reads=[("D", src.name, kc, tt)], writes=[("hb", i)])
                sumsq_accum(b, hb[i][:], ("hb", i), kc == 0, kc == KC - 1)
            rstd_from_bank(b, D, rstd, "rstd")
            held.discard(b)
            g0 = c.sm[gname]
            for kc in range(KC):
                i = rot("hb", NHB)
                kb.dma("io", hb[i][:], src.ap[kc * 128:(kc + 1) * 128, tok], reads=[("D", src.name, kc, tt)], writes=[("hb", i)])
                emit(kc, hb[i][:], ("hb", i), sm[:, g0 + kc:g0 + kc + 1])

        def xn_from_dram(src, tt, gname, xn):
            def emit(kc, h_ap, h_res, g_ap):
                kb.op("dve", lambda e: e.scalar_tensor_tensor(out=xn[:, kc, :], in0=h_ap, scalar=g_ap, in1=rstd[:],
                                                              op0=ALU.mult, op1=ALU.mult),
                      reads=[h_res, "sm", "rstd"], writes=[("A", "xn", kc)])
            norm_from_dram(src, tt, gname, emit)

        def resid_evac(src, dst, tt, scale):
            tok = slice(tt * TS, (tt + 1) * TS)

            def ev(mi, b):
                i = rot("hb", NHB)
                kb.dma("io", hb[i][:], src.ap[mi * 128:(mi + 1) * 128, tok], reads=[("D", src.name, mi, tt)], writes=[("hb", i)])
                kb.op("dve", lambda e: e.scalar_tensor_tensor(out=hb[i][:], in0=PS[b][:], scalar=scale, in1=hb[i][:],
                                                              op0=ALU.mult, op1=ALU.add),
                      reads=[("ps", b), ("hb", i)], writes=[("hb", i)])
                kb.dma("io", dst.ap[mi * 128:(mi + 1) * 128, tok], hb[i][:], reads=[("hb", i)], writes=[("D", dst.name, mi, tt)])
            return ev

        def linear(xsrc, xres, kcn, wv, mlist, evac, pre=3, kspl=1, blocked=False):
            kh = kcn // kspl
            blocks = [(mi, ks) for mi in range(len(mlist)) for ks in range(kspl)]
            views = {}

            def issue(bi):
                mi, ks = blocks[bi]
                s = rot("wslot", NW)
                c0, mw = mlist[mi]
                if blocked:
                    views[bi] = (s, load_wb(wv, c0 // 128, kh, mw, s))
                else:
                    views[bi] = (s, load_w(wv[ks * kh * 128:(ks + 1) * kh * 128, c0:c0 + mw], kh, mw, s))
            for bi in range(min(pre, len(blocks))):
                issue(bi)
            b = None
            for bi, (mi, ks) in enumerate(blocks):
                if bi + pre < len(blocks):
                    issue(bi + pre)
                s, w3 = views.pop(bi)
                c0, mw = mlist[mi]
                if ks == 0:
                    b = bank()
                kb.mm_group([(PS[b][0:mw, :], w3[:, kc, :], xsrc(ks * kh + kc)) for kc in range(kh)],
                            reads=[("w", s)] + [xres(ks * kh + kc) for kc in range(kh)], writes=[("ps", b)],
                            start=(ks == 0), stop=(ks == kspl - 1))
                if ks == kspl - 1:
                    evac(mi, b)

        def ffn(src, dst, gname, wib, wob):
            arena_switch()
            cv = Carver()
            NS = c.NS
            TT2 = NS * TS
            xn = cv(KC * TT2).rearrange("p (k t) -> p k t", t=TT2)
            act = cv(FQ * TT2).rearrange("p (k t) -> p k t", t=TT2)
            pre = 3
            for t2 in range(NT // TT2):
                for sub in range(NS):
                    tt = t2 * NS + sub

                    def emit(kc, h_ap, h_res, g_ap, sub=sub):
                        kb.op("dve", lambda e: e.scalar_tensor_tensor(out=xn[:, kc, sub * TS:(sub + 1) * TS], in0=h_ap, scalar=g_ap,
                                                                      in1=rstd[:], op0=ALU.mult, op1=ALU.mult),
                              reads=[h_res, "sm", "rstd"], writes=[("A", "xn", kc, sub)])
                    norm_from_dram(src, tt, gname, emit)
                for q in range(FSPL):
                    blocks = [(fc, wh) for fc in range(q * FQ, (q + 1) * FQ) for wh in (0, 1)]
                    views = {}

                    def issue(bi):
                        fc, wh = blocks[bi]
                        s_ = rot("wslot", NW)
                        views[bi] = (s_, load_wb(wib, wh * FC + fc, KC, 128, s_))
                    for bi in range(min(pre, len(blocks))):
                        issue(bi)
                    tis = [None] * NS
                    for bi, (fc, wh) in enumerate(blocks):
                        if bi + pre < len(blocks):
                            issue(bi + pre)
                        s_, w3 = views.pop(bi)
                        bks = [bank() for _ in range(NS)]
                        for sub in range(NS):
                            kb.mm_group([(PS[bks[sub]][:], w3[:, kc, :], xn[:, kc, sub * TS:(sub + 1) * TS]) for kc in range(KC)],
                                        reads=[("w", s_)] + [("A", "xn", kc, sub) for kc in range(KC)], writes=[("ps", bks[sub])])
                        for sub in range(NS):
                            b = bks[sub]
                            if wh == 0:
                                ti = rot("tmp", NTMP)
                                tis[sub] = ti
                                kb.op("act", lambda e: e.activation(out=tmp[ti][:], in_=PS[b][:], func=AF.Silu),
                                      reads=[("ps", b)], writes=[("tmp", ti)])
                            else:
                                ti = tis[sub]
                                kb.op("dve", lambda e: e.tensor_tensor(out=act[:, fc - q * FQ, sub * TS:(sub + 1) * TS], in0=tmp[ti][:],
                                                                       in1=PS[b][:], op=ALU.mult),
                                      reads=[("tmp", ti), ("ps", b)], writes=[("A", "act", fc - q * FQ, sub)])
                    views = {}

                    def issue2(dc):
                        s_ = rot("wslot", NW)
                        views[dc] = (s_, load_wb(wob, dc * FSPL + q, FQ, 128, s_))
                    for dc in range(min(pre, KC)):
                        issue2(dc)
                    for dc in range(KC):
                        if dc + pre < KC:
                            issue2(dc + pre)
                        s_, w3 = views.pop(dc)
                        bks = [bank() for _ in range(NS)]
                        for sub in range(NS):
                            kb.mm_group([(PS[bks[sub]][:], w3[:, kc, :], act[:, kc, sub * TS:(sub + 1) * TS]) for kc in range(FQ)],
                                        reads=[("w", s_)] + [("A", "act", kc, sub) for kc in range(FQ)], writes=[("ps", bks[sub])])
                        for sub in range(NS):
                            resid_evac(src if q == 0 else dst, dst, t2 * NS + sub, 0.5)(dc, bks[sub])

        def mla(j, gname):
            w_in, w_uq, w_ukv, w_o = W["min", j], W["muq", j], W["mukv", j], W["mo", j]
            scale = (128 + 64) ** -0.5
            arena_switch()
            cv = Carver()
            xn = cv(KC * TS).rearrange("p (k t) -> p k t", t=TS)
            MQ = max(QLC, KVC)
            cf = cv(MQ * TS * 2).bitcast(F32).rearrange("p (k t) -> p k t", t=TS)
            cn = cv(MQ * TS).rearrange("p (k t) -> p k t", t=TS)
            krt = cv(TS)[0:64, :]
            for tt in range(NTT):
                tok = slice(tt * TS, (tt + 1) * TS)
                latd = DT(lat_t[j][tt].ap(), f"lat{j}_{tt}")
                xn_from_dram(hT, tt, gname, xn)
                state = {"bq": bank(hold=True)}
                pend = []
                mlist = [(cc * 128, 128) for cc in range(QLC + KVC)] + [(QL + KVL, 64), (QL + KVL + 64, 64)]

                def fin_norm(nch, gkey, nfeat, dst_t, dtok):
                    bq = state["bq"]
                    rstd_from_bank(bq, nfeat, rstd2, "rstd2")
                    held.discard(bq)
                    g0 = c.sm[gkey]
                    for cc in range(nch):
                        kb.op("dve", lambda e: e.scalar_tensor_tensor(out=cn[:, cc, :], in0=cf[:, cc, :],
                                                                      scalar=sm[:, g0 + cc:g0 + cc + 1], in1=rstd2[:],
                                                                      op0=ALU.mult, op1=ALU.mult),
                              reads=[("A", "cf", cc), "sm", "rstd2"], writes=[("A", "cn", cc)])
                        kb.dma("io", dst_t.ap[cc * 128:(cc + 1) * 128, dtok], cn[:, cc, :],
                               reads=[("A", "cn", cc)], writes=[("D", dst_t.name, cc, tt)])

                def newbank():
                    state["bq"] = bank(hold=True)

                def ev(mi, b):
                    for p in pend:
                        p()
                    pend.clear()
                    if mi < QLC + KVC:
                        cc = mi if mi < QLC else mi - QLC
                        nch = QLC if mi < QLC else KVC
                        kb.op("act", lambda e: e.activation(out=cf[:, cc, :], in_=PS[b][:], func=AF.Copy),
                              reads=[("ps", b)], writes=[("A", "cf", cc)])
                        bqq = state["bq"]
                        pend.append(lambda: sumsq_accum(bqq, cf[:, cc, :], ("A", "cf", cc), cc == 0, cc == nch - 1))
                        if mi == QLC - 1:
                            pend.append(lambda: fin_norm(QLC, ("qn", j), QL, cqn_d, tok))
                            pend.append(newbank)
                        if mi == QLC + KVC - 1:
                            pend.append(lambda: fin_norm(KVC, ("kvn", j), KVL, latd, slice(0, TS)))
                    elif mi == QLC + KVC:
                        ti = rot("tmp", NTMP)
                        state["ta"] = ti
                        kb.op("dve", lambda e: e.tensor_tensor(out=tmp[ti][0:64, :], in0=PS[b][0:64, :], in1=cs1[:, tok], op=ALU.mult),
                              reads=[("ps", b), CS1], writes=[("tmp", ti)])
                    else:
                        ti = state["ta"]
                        t2 = rot("tmp", NTMP)
                        kb.op("dve", lambda e: e.tensor_tensor(out=tmp[t2][0:64, :], in0=PS[b][0:64, :], in1=cs2[:, tok], op=ALU.mult),
                              reads=[("ps", b), CS2], writes=[("tmp", t2)])
                        kb.op("dve", lambda e: e.tensor_tensor(out=krt, in0=tmp[ti][0:64, :], in1=tmp[t2][0:64, :], op=ALU.add),
                              reads=[("tmp", ti), ("tmp", t2)], writes=[("A", "krt")])
                        kb.dma("io", latd.ap[KVL:KVL + 64, :], krt, reads=[("A", "krt")], writes=[("D", latd.name, "kr", tt)])
                linear(lambda kc: xn[:, kc, :], lambda kc: ("A", "xn", kc), KC, w_in, mlist, ev)
                for p in pend:
                    p()
                pend.clear()
                lat_res = [("D", latd.name, cc, tt) for cc in range(KVC)] + [("D", latd.name, "kr", tt)]
                kb.collective(lat_t[j][tt].ap().opt(), latg_t[j][tt].ap().opt(), groups, reads=lat_res, writes=[("latg", j, tt)])
            latg = [latg_t[j][t].ap() for t in range(NTT)]
            arena_switch()
            cv = Carver()
            ckr = [cv(KVC * TS).rearrange("p (k t) -> p k t", t=TS) for _ in range(2)]
            cqr = [cv(QLC * TS).rearrange("p (k t) -> p k t", t=TS) for _ in range(2)]
            kr_all = cv(S)
            kT = cv(S)
            Vt = cv(S).rearrange("p (k v) -> p k v", v=128)
            qn = cv(NT)
            qr = cv(NT)
            mk = cv(G * 2 * TS).rearrange("p (r t) -> p r t", t=2 * TS)
            for r in range(G):
                for tl in range(NTT):
                    kb.dma("io", kr_all[0:64, r * NT + tl * TS:r * NT + (tl + 1) * TS], latg[tl][r * c.LATR + KVL:(r + 1) * c.LATR, :],
                           reads=[("latg", j, tl)], writes=[("A", "kr_all", r, tl)])
            kb.dma("io", mk.rearrange("p r t -> p (r t)"), masks, writes=[("A", "mk")])
            NKB = S // 128
            KPR = NT // TS
            for h in range(H):
                s1 = rot("wslot", NW)
                wkv = load_w(w_ukv[:, h * 256:(h + 1) * 256], KVC, 256, s1)
                s2 = rot("wslot", NW)
                wq = load_w(w_uq[:, h * 256:(h + 1) * 256], QLC, 256, s2)
                for ks in range(S // TS):
                    r, tl = ks // KPR, ks % KPR
                    ci = rot("ckr", 2)
                    kb.dma("io", ckr[ci], latg[tl][r * c.LATR:r * c.LATR + KVL, :].rearrange("(k p) t -> p k t", p=128),
                           reads=[("latg", j, tl)], writes=[("A", "ckr", ci)])
                    b = bank()
                    kb.mm_group([(PS[b][:], wkv[:, kc, 0:128], ckr[ci][:, kc, :]) for kc in range(KVC)],
                                reads=[("w", s1), ("A", "ckr", ci)], writes=[("ps", b)])
                    kb.op("act", lambda e: e.activation(out=kT[:, ks * TS:(ks + 1) * TS], in_=PS[b][:], func=AF.Copy),
                          reads=[("ps", b)], writes=[("A", "kT", ks)])
                    b2 = bank()
                    for jj in range(4):
                        kb.mm_group([(PS[b2][:, jj * 128:(jj + 1) * 128], ckr[ci][:, kc, jj * 128:(jj + 1) * 128], wkv[:, kc, 128:256])
                                     for kc in range(KVC)], reads=[("w", s1), ("A", "ckr", ci)], writes=[("ps", b2)])
                    dstv = Vt[:, ks * 4:(ks + 1) * 4, :].rearrange("p k v -> p (k v)")
                    kb.op("dve", lambda e: e.tensor_copy(out=dstv, in_=PS[b2][:]), reads=[("ps", b2)], writes=[("A", "V", ks)])
                for qs in range(NTT):
                    tok = slice(qs * TS, (qs + 1) * TS)
                    ci = rot("cqr", 2)
                    kb.dma("io", cqr[ci], cqn_d.ap[:, tok].rearrange("(k p) t -> p k t", p=128),
                           reads=[("D", cqn_d.name, cc, qs) for cc in range(QLC)], writes=[("A", "cqr", ci)])
                    b = bank()
                    kb.mm_group([(PS[b][:], wq[:, kc, 0:128], cqr[ci][:, kc, :]) for kc in range(QLC)],
                                reads=[("w", s2), ("A", "cqr", ci)], writes=[("ps", b)])
                    kb.op("act", lambda e: e.activation(out=qn[:, tok], in_=PS[b][:], func=AF.Copy),
                          reads=[("ps", b)], writes=[("A", "qn", qs)])
                    ba = bank()
                    kb.mm_group([(PS[ba][0:64, :], wq[:, kc, 128:192], cqr[ci][:, kc, :]) for kc in range(QLC)],
                                reads=[("w", s2), ("A", "cqr", ci)], writes=[("ps", ba)])
                    bb = bank()
                    kb.mm_group([(PS[bb][0:64, :], wq[:, kc, 192:256], cqr[ci][:, kc, :]) for kc in range(QLC)],
                                reads=[("w", s2), ("A", "cqr", ci)], writes=[("ps", bb)])
                    t1, t2 = rot("tmp", NTMP), rot("tmp", NTMP)
                    kb.op("dve", lambda e: e.tensor_tensor(out=tmp[t1][0:64, :], in0=PS[ba][0:64, :], in1=cs1[:, tok], op=ALU.mult),
                          reads=[("ps", ba), CS1], writes=[("tmp", t1)])
                    kb.op("dve", lambda e: e.tensor_tensor(out=tmp[t2][0:64, :], in0=PS[bb][0:64, :], in1=cs2[:, tok], op=ALU.mult),
                          reads=[("ps", bb), CS2], writes=[("tmp", t2)])
                    kb.op("dve", lambda e: e.tensor_tensor(out=qr[0:64, tok], in0=tmp[t1][0:64, :], in1=tmp[t2][0:64, :], op=ALU.add),
                          reads=[("tmp", t1), ("tmp", t2)], writes=[("A", "qr", qs)])
                for qs in range(NTT):
                    tok = slice(qs * TS, (qs + 1) * TS)
                    bo, bl = bank(hold=True), bank(hold=True)
                    prev = None

                    def s_stage(kbk):
                        b = bank()
                        r = kbk // (NT // 128)
                        kb.mm_group([(PS[b][:], kT[:, kbk * 128:(kbk + 1) * 128], qn[:, tok]),
                                     (PS[b][:], kr_all[0:64, kbk * 128:(kbk + 1) * 128], qr[0:64, tok])],
                                    reads=[("A", "kT", kbk // 4), ("A", "qn", qs), ("A", "qr", qs), ("A", "kr_all", r, (kbk // 4) % KPR)],
                                    writes=[("ps", b)])
                        pi = rot("pt", 3)
                        kb.op("act", lambda e: e.activation(out=pt[pi][:], in_=PS[b][:], func=AF.Exp,
                                                            bias=C(c.c_kbias + r), scale=scale),
                              reads=[("ps", b), "ct"], writes=[("pt", pi)])
                        dd = (kbk % (NT // 128)) - 4 * qs
                        if dd >= 0:
                            o_ = TS - 128 * min(dd, 4)
                            kb.op("dve", lambda e: e.tensor_tensor(out=pt[pi][:], in0=pt[pi][:], in1=mk[:, r, o_:o_ + TS], op=ALU.mult),
                                  reads=[("pt", pi), ("A", "mk")], writes=[("pt", pi)])
                        return pi

                    def pv_stage(kbk, pi):
                        kb.mm_group([(PS[bo][:], Vt[:, kbk, :], pt[pi][:])], reads=[("A", "V", kbk // 4), ("pt", pi)],
                                    writes=[("ps", bo)], start=(kbk == 0), stop=(kbk == NKB - 1))
                        kb.mm_group([(PS[bl][:], ones[:], pt[pi][:])], reads=["ones", ("pt", pi)],
                                    writes=[("ps", bl)], start=(kbk == 0), stop=(kbk == NKB - 1))
                    for kbk in range(NKB):
                        pi = s_stage(kbk)
                        if prev is not None:
                            pv_stage(*prev)
                        prev = (kbk, pi)
                    pv_stage(*prev)
                    ti = rot("tmp", NTMP)
                    kb.op("dve", lambda e: e.reciprocal(out=tmp[ti][:], in_=PS[bl][:]), reads=[("ps", bl)], writes=[("tmp", ti)])
                    oi = rot("ot", 2)
                    kb.op("dve", lambda e: e.tensor_tensor(out=ot[oi][:], in0=PS[bo][:], in1=tmp[ti][:], op=ALU.mult),
                          reads=[("ps", bo), ("tmp", ti)], writes=[("ot", oi)])
                    held.discard(bo)
                    held.discard(bl)
                    kb.dma("io", o_d.ap[h * 128:(h + 1) * 128, tok], ot[oi][:], reads=[("ot", oi)], writes=[("D", o_d.name, h, qs)])
            arena_switch()
            cv = Carver()
            on = cv(KC * TS).rearrange("p (k t) -> p k t", t=TS)
            for tt in range(NTT):
                tok = slice(tt * TS, (tt + 1) * TS)
                for kc in range(H):
                    kb.dma("io", on[:, kc, :], o_d.ap[kc * 128:(kc + 1) * 128, tok],
                           reads=[("D", o_d.name, kc, tt)], writes=[("A", "xn", kc)])
                linear(lambda kc: on[:, kc, :], lambda kc: ("A", "xn", kc), H, w_o,
                       [(dc * 128, 128) for dc in range(KC)], resid_evac(hT, hT, tt, 1.0), blocked=True)

        def rglru(j, gname):
            w_in, w_ga, w_gx, w_out = W["rin", j], W["rga", j], W["rgx", j], W["rout", j]
            arena_switch()
            cv = Carver()
            xn = cv(KC * TS).rearrange("p (k t) -> p k t", t=TS)
            zn = xn
            xbb = cv(2 * TS).rearrange("p (k t) -> p k t", t=TS)
            cm = c.sm
            apc = sm[:, cm["ap", j]:cm["ap", j] + LC]
            kb.op("act", lambda e: e.activation(out=sp8[:], in_=apc, func=AF.Exp, scale=-1.0), reads=["sm"], writes=["sp8"])
            kb.op("act", lambda e: e.activation(out=sp8[:], in_=sp8[:], func=AF.Ln, bias=C(c.c_one), scale=1.0),
                  reads=["sp8", "ct"], writes=["sp8"])
            kb.op("dve", lambda e: e.tensor_scalar(out=sp8[:], in0=sp8[:], scalar1=-8.0, scalar2=None, op0=ALU.mult),
                  reads=["sp8"], writes=["sp8"])
            kb.op("dve", lambda e: e.memset(halo_t[:], 0.0), writes=["halo_t"])
            for tt in range(NTT):
                tok = slice(tt * TS, (tt + 1) * TS)
                xn_from_dram(hT, tt, gname, xn)

                def ev(mi, b):
                    if mi < LC:
                        t1, t2 = rot("tmp", NTMP), rot("tmp", NTMP)
                        kb.op("act", lambda e: e.activation(out=tmp[t1][:], in_=PS[b][:], func=AF.Square),
                              reads=[("ps", b)], writes=[("tmp", t1)])
                        kb.op("dve", lambda e: e.tensor_scalar(out=tmp[t1][:], in0=tmp[t1][:], scalar1=0.044715, scalar2=1.0,
                                                               op0=ALU.mult, op1=ALU.add), reads=[("tmp", t1)], writes=[("tmp", t1)])
                        kb.op("dve", lambda e: e.tensor_tensor(out=tmp[t1][:], in0=tmp[t1][:], in1=PS[b][:], op=ALU.mult),
                              reads=[("tmp", t1), ("ps", b)], writes=[("tmp", t1)])
                        kb.op("act", lambda e: e.activation(out=tmp[t1][:], in_=tmp[t1][:], func=AF.Sigmoid,
                                                            scale=2.0 * math.sqrt(2.0 / math.pi)),
                              reads=[("tmp", t1)], writes=[("tmp", t1)])
                        kb.op("dve", lambda e: e.tensor_tensor(out=tmp[t2][:], in0=tmp[t1][:], in1=PS[b][:], op=ALU.mult),
                              reads=[("tmp", t1), ("ps", b)], writes=[("tmp", t2)])
                        kb.dma("io", y_d.ap[mi * 128:(mi + 1) * 128, tok], tmp[t2][:], reads=[("tmp", t2)],
                               writes=[("D", y_d.name, mi, tt)])
                    else:
                        cc = mi - LC
                        t1 = rot("tmp", NTMP)
                        kb.op("act", lambda e: e.activation(out=tmp[t1][:], in_=PS[b][:], func=AF.Copy),
                              reads=[("ps", b)], writes=[("tmp", t1)])
                        kb.dma("io", xp_d.ap[cc * 128:(cc + 1) * 128, 4 + tt * TS:4 + (tt + 1) * TS], tmp[t1][:],
                               reads=[("tmp", t1)], writes=[("D", xp_d.name, cc, tt)])
                        if tt == NTT - 1:
                            kb.op("dve", lambda e: e.tensor_copy(out=halo_t[:, cc, 1:4], in_=tmp[t1][:, TS - 3:TS]),
                                  reads=[("tmp", t1)], writes=["halo_t"])
                linear(lambda kc: xn[:, kc, :], lambda kc: ("A", "xn", kc), KC, w_in,
                       [(m * 128, 128) for m in range(2 * LC)], ev, blocked=True)
            kb.dma("io", halo_t_d[j].ap(), halo_t[:].rearrange("p k f -> p (k f)"), reads=["halo_t"], writes=[("halo_d", j)])
            kb.collective(halo_t_d[j].ap().opt(), halog_t[j].ap().opt(), groups, reads=[("halo_d", j)], writes=[("halog", j)])
            kb.dma("io", hg[:], halog_t[j].ap().rearrange("(r p) f -> p r f", p=128), reads=[("halog", j)], writes=["hg"])
            kb.op("dve", lambda e: e.tensor_scalar(out=hs[:], in0=hg[:, 0, :], scalar1=C(c.c_selprev), scalar2=None, op0=ALU.mult),
                  reads=["hg", "ct"], writes=["hs"])
            for r in range(1, G):
                kb.op("dve", lambda e: e.scalar_tensor_tensor(out=hs[:], in0=hg[:, r, :], scalar=C(c.c_selprev + r), in1=hs[:],
                                                              op0=ALU.mult, op1=ALU.add), reads=["hg", "ct", "hs"], writes=["hs"])
            for cc in range(LC):
                kb.dma("io", xp_d.ap[cc * 128:(cc + 1) * 128, 0:4], hs[:, cc * 4:(cc + 1) * 4], reads=["hs"],
                       writes=[("D", xp_d.name, cc, "halo")])
            cw0, cb0, gab0, gxb0 = cm["cw", j], cm["cb", j], cm["gab", j], cm["gxb", j]
            for tt in range(NTT):
                tok = slice(tt * TS, (tt + 1) * TS)
                for n in range(c.NB):
                    sa = rot("wslot", NW)
                    wa = load_w(w_ga[n * 256:(n + 1) * 256, :], 2, 256, sa)
                    sx = rot("wslot", NW)
                    wx = load_w(w_gx[n * 256:(n + 1) * 256, :], 2, 256, sx)
                    for ci in range(2):
                        cc = 2 * n + ci
                        xi = rot("xpt", 2)
                        rds = [("D", xp_d.name, cc, tt), ("D", xp_d.name, cc, "halo" if tt == 0 else tt - 1)]
                        kb.dma("io", xpt[xi][:, 0:TS + 3], xp_d.ap[cc * 128:(cc + 1) * 128, 1 + tt * TS:4 + (tt + 1) * TS],
                               reads=rds, writes=[("xpt", xi)])
                        xb = xbf[ci]
                        kb.op("dve", lambda e: e.tensor_scalar(out=xb[:], in0=xpt[xi][:, 3:TS + 3],
                                                               scalar1=sm[:, cw0 + 3 * LC + cc:cw0 + 3 * LC + cc + 1],
                                                               scalar2=sm[:, cb0 + cc:cb0 + cc + 1], op0=ALU.mult, op1=ALU.add),
                              reads=[("xpt", xi), "sm"], writes=[("xbf", ci)])
                        for wi in (2, 1, 0):
                            kb.op("dve", lambda e: e.scalar_tensor_tensor(out=xb[:], in0=xpt[xi][:, wi:TS + wi],
                                                                          scalar=sm[:, cw0 + wi * LC + cc:cw0 + wi * LC + cc + 1],
                                                                          in1=xb[:], op0=ALU.mult, op1=ALU.add),
                                  reads=[("xpt", xi), "sm", ("xbf", ci)], writes=[("xbf", ci)])
                        kb.op("act", lambda e: e.activation(out=xbb[:, ci, :], in_=xb[:], func=AF.Copy),
                              reads=[("xbf", ci)], writes=[("A", "xbb", ci)])
                    for jc in range(2):
                        cc = 2 * n + jc
                        ba, bx = bank(), bank()
                        kb.mm_group([(PS[ba][:], wa[:, ic, jc * 128:(jc + 1) * 128], xbb[:, ic, :]) for ic in range(2)],
                                    reads=[("w", sa), ("A", "xbb", 0), ("A", "xbb", 1)], writes=[("ps", ba)])
                        kb.mm_group([(PS[bx][:], wx[:, ic, jc * 128:(jc + 1) * 128], xbb[:, ic, :]) for ic in range(2)],
                                    reads=[("w", sx), ("A", "xbb", 0), ("A", "xbb", 1)], writes=[("ps", bx)])
                        ga, gi, gb_ = gt[0], gt[1], gt[2]
                        kb.op("act", lambda e: e.activation(out=ga[:], in_=PS[ba][:], func=AF.Sigmoid,
                                                            bias=sm[:, gab0 + cc:gab0 + cc + 1], scale=1.0),
                              reads=[("ps", ba), "sm"], writes=[("gt", 0)])
                        kb.op("act", lambda e: e.activation(out=gi[:], in_=PS[bx][:], func=AF.Sigmoid,
                                                            bias=sm[:, gxb0 + cc:gxb0 + cc + 1], scale=1.0),
                              reads=[("ps", bx), "sm"], writes=[("gt", 1)])
                        kb.op("act", lambda e: e.activation(out=ga[:], in_=ga[:], func=AF.Exp, scale=sp8[:, cc:cc + 1]),
                              reads=[("gt", 0), "sp8"], writes=[("gt", 0)])
                        kb.op("act", lambda e: e.activation(out=gb_[:], in_=ga[:], func=AF.Square), reads=[("gt", 0)], writes=[("gt", 2)])
                        kb.op("act", lambda e: e.activation(out=gb_[:], in_=gb_[:], func=AF.Sqrt, bias=C(c.c_one), scale=-1.0),
                              reads=[("gt", 2), "ct"], writes=[("gt", 2)])
                        kb.op("dve", lambda e: e.tensor_tensor(out=gi[:], in0=gi[:], in1=xbf[jc][:], op=ALU.mult),
                              reads=[("gt", 1), ("xbf", jc)], writes=[("gt", 1)])
                        kb.op("dve", lambda e: e.tensor_tensor(out=gi[:], in0=gi[:], in1=gb_[:], op=ALU.mult),
                              reads=[("gt", 1), ("gt", 2)], writes=[("gt", 1)])
                        t1, t2 = rot("tmp", NTMP), rot("tmp", NTMP)
                        init_h = 0.0 if tt == 0 else st_h[:, LC + cc:LC + cc + 1]
                        init_p = 1.0 if tt == 0 else st_h[:, cc:cc + 1]
                        kb.op("dve", lambda e: e.tensor_tensor_scan(out=tmp[t1][:], data0=ga[:], data1=gi[:], initial=init_h,
                                                                    op0=ALU.mult, op1=ALU.add),
                              reads=[("gt", 0), ("gt", 1), ("st_h", cc)], writes=[("tmp", t1)])
                        kb.op("dve", lambda e: e.tensor_tensor_scan(out=tmp[t2][:], data0=ga[:], data1=zer[:], initial=init_p,
                                                                    op0=ALU.mult, op1=ALU.add),
                              reads=[("gt", 0), "zer", ("st_p", cc)], writes=[("tmp", t2)])
                        kb.op("act", lambda e: e.activation(out=st_h[:, LC + cc:LC + cc + 1], in_=tmp[t1][:, TS - 1:TS], func=AF.Copy),
                              reads=[("tmp", t1)], writes=[("st_h", cc)])
                        kb.op("act", lambda e: e.activation(out=st_h[:, cc:cc + 1], in_=tmp[t2][:, TS - 1:TS], func=AF.Copy),
                              reads=[("tmp", t2)], writes=[("st_p", cc)])
                        kb.dma("io", hl_d.ap[cc * 128:(cc + 1) * 128, tok], tmp[t1][:], reads=[("tmp", t1)],
                               writes=[("D", hl_d.name, cc, tt)])
                        kb.dma("io", P_d.ap[cc * 128:(cc + 1) * 128, tok], tmp[t2][:], reads=[("tmp", t2)],
                               writes=[("D", P_d.name, cc, tt)])
            st_res = [("st_h", cc) for cc in range(LC)] + [("st_p", cc) for cc in range(LC)]
            kb.dma("io", car_t[j].ap(), st_h[:], reads=st_res, writes=[("car_d", j)])
            kb.collective(car_t[j].ap().opt(), carg_t[j].ap().opt(), groups, reads=[("car_d", j)], writes=[("carg", j)])
            kb.dma("io", cg[:], carg_t[j].ap().rearrange("(r p) f -> p r f", p=128), reads=[("carg", j)], writes=["cg"])
            kb.op("dve", lambda e: e.memset(hin[:], 0.0), writes=["hin"])
            for r in range(G):
                kb.op("dve", lambda e: e.tensor_tensor(out=dlt[:], in0=cg[:, r, 0:LC], in1=hin[:], op=ALU.mult),
                      reads=["cg", "hin"], writes=["dlt"])
                kb.op("dve", lambda e: e.tensor_tensor(out=dlt[:], in0=dlt[:], in1=cg[:, r, LC:2 * LC], op=ALU.add),
                      reads=["cg", "dlt"], writes=["dlt"])
                kb.op("dve", lambda e: e.tensor_tensor(out=dlt[:], in0=dlt[:], in1=hin[:], op=ALU.subtract),
                      reads=["hin", "dlt"], writes=["dlt"])
                kb.op("dve", lambda e: e.scalar_tensor_tensor(out=hin[:], in0=dlt[:], scalar=C(c.c_selbef + r), in1=hin[:],
                                                              op0=ALU.mult, op1=ALU.add), reads=["dlt", "hin", "ct"], writes=["hin"])
            for tt in range(NTT):
                tok = slice(tt * TS, (tt + 1) * TS)
                for cc in range(LC):
                    i1, i2, i3 = rot("hb", NHB), rot("hb", NHB), rot("hb", NHB)
                    kb.dma("io", hb[i1][:], hl_d.ap[cc * 128:(cc + 1) * 128, tok], reads=[("D", hl_d.name, cc, tt)], writes=[("hb", i1)])
                    kb.dma("io", hb[i2][:], P_d.ap[cc * 128:(cc + 1) * 128, tok], reads=[("D", P_d.name, cc, tt)], writes=[("hb", i2)])
                    kb.dma("io", hb[i3][:], y_d.ap[cc * 128:(cc + 1) * 128, tok], reads=[("D", y_d.name, cc, tt)], writes=[("hb", i3)])
                    kb.op("dve", lambda e: e.scalar_tensor_tensor(out=hb[i2][:], in0=hb[i2][:], scalar=hin[:, cc:cc + 1], in1=hb[i1][:],
                                                                  op0=ALU.mult, op1=ALU.add),
                          reads=[("hb", i1), ("hb", i2), "hin"], writes=[("hb", i2)])
                    kb.op("dve", lambda e: e.tensor_tensor(out=zn[:, cc, :], in0=hb[i2][:], in1=hb[i3][:], op=ALU.mult),
                          reads=[("hb", i2), ("hb", i3)], writes=[("A", "xn", cc)])
                linear(lambda kc: zn[:, kc, :], lambda kc: ("A", "xn", kc), LC, w_out,
                       [(dc * 128, 128) for dc in range(KC)], resid_evac(hT, hT, tt, 1.0), blocked=True)

        def final_norm():
            for tt in range(NTT):
                tok = slice(tt * TS, (tt + 1) * TS)

                def emit(kc, h_ap, h_res, g_ap):
                    ti = rot("tmp", NTMP)
                    kb.op("dve", lambda e: e.scalar_tensor_tensor(out=tmp[ti][:], in0=h_ap, scalar=g_ap, in1=rstd[:],
                                                                  op0=ALU.mult, op1=ALU.mult),
                          reads=[h_res, "sm", "rstd"], writes=[("tmp", ti)])
                    kb.dma("io", outT[kc * 128:(kc + 1) * 128, tok], tmp[ti][:], reads=[("tmp", ti)], writes=[("OUT", kc, tt)])
                norm_from_dram(hT, tt, "gf", emit)

        for i in range(c.DEPTH):
            ffn(xT if i == 0 else hT, hT, ("g1", i), W["f1i", i], W["f1o", i])
            if i % 2 == 0:
                mla(i // 2, ("gm", i))
            else:
                rglru(i // 2, ("gm", i))
            ffn(hT, hT, ("g2", i), W["f2i", i], W["f2o", i])
        final_norm()
        kb.finish([("OUT", kc, tt) for kc in range(KC) for tt in range(NTT)])
        print("instructions:", kb.ninst, "arena KB:", ARENA * 2 / 1024)
    return nc


def pc(v):
    return np.ascontiguousarray(v.reshape(-1, 128).T)


def blk_cols(w):
    K, M = w.shape
    return np.ascontiguousarray(w.reshape(K // 128, 128, M // 128, 128).transpose(2, 1, 0, 3)).reshape(M // 128, 128, K)


def blk_out(w, fspl):
    Fd, Dd = w.shape
    fq = Fd // fspl // 128
    t = w.reshape(fspl, fq, 128, Dd // 128, 128).transpose(3, 0, 2, 1, 4)
    return np.ascontiguousarray(t).reshape((Dd // 128) * fspl, 128, fq * 128)


def host_inputs(c, inp):
    D, F, H, QL, KVL, NT, G = c.D, c.F, c.H, c.QL, c.KVL, c.NT, c.G
    shared = {}
    sm = np.zeros((128, c.NSM), np.float32)

    def put(key, arr2d):
        o = c.sm[key]
        sm[:, o:o + arr2d.shape[1]] = arr2d
    for i in range(c.DEPTH):
        put(("g1", i), pc(inp["norm_ffn1"][i])); put(("gm", i), pc(inp["norm_mix"][i])); put(("g2", i), pc(inp["norm_ffn2"][i]))
        shared[f"f1i{i}"] = blk_cols(inp["ffn1_in"][i]); shared[f"f1o{i}"] = blk_out(inp["ffn1_out"][i], c.FSPL)
        shared[f"f2i{i}"] = blk_cols(inp["ffn2_in"][i]); shared[f"f2o{i}"] = blk_out(inp["ffn2_out"][i], c.FSPL)
    for j in range(c.NA):
        put(("qn", j), pc(inp["mla_q_norm"][j])); put(("kvn", j), pc(inp["mla_kv_norm"][j]))
        wi = inp["mla_in"][j]
        r0 = QL + KVL
        shared[f"min{j}"] = np.concatenate([wi, wi[:, r0 + 32:r0 + 64], wi[:, r0:r0 + 32]], axis=1)
        wq = inp["mla_w_uq"][j].reshape(QL, H, 192)
        shared[f"muq{j}"] = np.concatenate([wq, wq[:, :, 160:192], wq[:, :, 128:160]], axis=2).reshape(QL, H * 256)
        shared[f"mukv{j}"] = inp["mla_w_ukv"][j]
        shared[f"mo{j}"] = blk_cols(inp["mla_w_o"][j])
    for j in range(c.NR):
        cw = inp["rg_conv_w"][j]
        put(("cw", j), np.concatenate([pc(cw[w]) for w in range(4)], axis=1))
        put(("cb", j), pc(inp["rg_conv_b"][j])); put(("gab", j), pc(inp["rg_gate_a_b"][j].reshape(-1)))
        put(("gxb", j), pc(inp["rg_gate_x_b"][j].reshape(-1))); put(("ap", j), pc(inp["rg_a_param"][j]))
        shared[f"rin{j}"] = blk_cols(inp["rg_in"][j])
        shared[f"rga{j}"] = inp["rg_gate_a_w"][j].reshape(c.NB * 256, 256)
        shared[f"rgx{j}"] = inp["rg_gate_x_w"][j].reshape(c.NB * 256, 256)
        shared[f"rout{j}"] = blk_cols(inp["rg_out"][j])
    put("gf", pc(inp["norm_final"]))
    shared["smalls"] = sm
    shared = {k: np.ascontiguousarray(v, dtype=np.float32) for k, v in shared.items()}
    p = np.arange(128)
    invf = (10000.0 ** (-(np.arange(32, dtype=np.float32)) / 32.0)).astype(np.float32)
    xx = np.arange(2 * TS)
    base = (p[:, None] <= xx[None, :] - TS).astype(np.float32)
    in_maps = []
    x = inp["x"]
    for core in range(c.NCORES):
        b, r = core // G, core % G
        cst = np.zeros((128, c.NCST), np.float32)
        cst[:, c.c_invf] = invf[p % 32]
        cst[:, c.c_sgn] = np.where(p % 64 < 32, -1.0, 1.0)
        cst[:, c.c_eps] = 1e-6
        cst[:, c.c_one] = 1.0
        for rr_ in range(G):
            cst[:, c.c_kbias + rr_] = 0.0 if rr_ <= r else -30000.0
            cst[:, c.c_selprev + rr_] = 1.0 if rr_ == r - 1 else 0.0
            cst[:, c.c_selbef + rr_] = 1.0 if rr_ < r else 0.0
        mk = np.ones((128, G, 2 * TS), np.float32)
        mk[:, r] = base
        m = {"xT": np.ascontiguousarray(x[b, r * NT:(r + 1) * NT, :].T),
             "pos": np.ascontiguousarray(inp["positions"][b, r * NT:(r + 1) * NT].reshape(1, NT).astype(np.int32)),
             "cst": cst, "masks": mk.reshape(128, -1).astype(ml_dtypes.bfloat16)}
        m.update(shared)
        in_maps.append(m)
    return in_maps


def run(c, inp):
    nc = build(c)
    in_maps = host_inputs(c, inp)
    res = run_bass_kernel_spmd(nc, in_maps, core_ids=list(range(c.NCORES)))
    B = c.NCORES // c.G
    out = np.zeros((B, c.S, c.D), np.float32)
    for core in range(c.NCORES):
        b, r = core // c.G, core % c.G
        out[b, r * c.NT:(r + 1) * c.NT, :] = res.results[core]["outT"].T
    return out


def kernel(**inputs):
    inp = {k: np.asarray(v) for k, v in inputs.items()}
    return run(Cfg(), inp)
```
